# Optimizing a Trainium2 kernel written in Bass

```python
import jax, jax.numpy as jnp
from jax import lax
import numpy as np

D_MODEL = 2048
BATCH = 2
SEQ = 4096
DEPTH = 2

N_MIXERS = 2
N_CONV_LAYERS = (DEPTH + 1) // 2
N_RWKV_LAYERS = DEPTH // 2
CONV_WIDTH = 3
HEAD_SIZE = 64
N_HEADS = D_MODEL // HEAD_SIZE
D_DECAY_LORA = 96
D_AAA_LORA = 96
D_GATE_LORA = 256
D_FF = 5632
RMS_EPS = 1e-6
GN_EPS = 64e-5
L2_EPS = 1e-12

kernel_name = "hybrid_shortconv_rwkv7_convffn"


def rmsnorm(x, g):
    xf = x.astype(jnp.float32)
    y = xf * lax.rsqrt(jnp.mean(xf * xf, axis=-1, keepdims=True) + RMS_EPS)
    return (y * g.astype(jnp.float32)).astype(x.dtype)


def causal_dwconv(x, w):
    c = x.shape[-1]
    return lax.conv_general_dilated(
        x, w[:, None, :].astype(x.dtype), window_strides=(1,),
        padding=[(w.shape[0] - 1, 0)],
        dimension_numbers=("NWC", "WIO", "NWC"), feature_group_count=c)


def short_conv_mixer(x, w_in, conv_w, w_out):
    b, c, h = jnp.split(x @ w_in, 3, axis=-1)
    return (b * causal_dwconv(c * h, conv_w)) @ w_out


def rwkv7_time_mix(x, mu, w_r, w_k, w_v, w_o, w0, w1, w2, a0, a1, a2,
                   g1, g2, k_k, k_a, r_k, gn_g, gn_b):
    bsz, t, d = x.shape
    h, n = N_HEADS, HEAD_SIZE
    xx = jnp.pad(x, ((0, 0), (1, 0), (0, 0)))[:, :-1] - x
    xm = x[None] + xx[None] * mu[:, None, None, :]
    xr, xw, xk, xv, xa, xg = xm[0], xm[1], xm[2], xm[3], xm[4], xm[5]

    r = xr @ w_r
    k = xk @ w_k
    v = xv @ w_v
    w_log = -jax.nn.softplus(-(w0 + jnp.tanh(xw @ w1) @ w2)) - 0.5
    decay = jnp.exp(-jnp.exp(w_log.astype(jnp.float32)))
    a = jax.nn.sigmoid(a0 + (xa @ a1) @ a2)
    g = jax.nn.sigmoid(xg @ g1) @ g2

    kk = (k * k_k).reshape(bsz, t, h, n).astype(jnp.float32)
    kk = kk / jnp.maximum(jnp.linalg.norm(kk, axis=-1, keepdims=True), L2_EPS)
    k = k * (1.0 + (a - 1.0) * k_a)

    def heads(z):
        return z.reshape(bsz, t, h, n).astype(jnp.float32)

    rh, kh, vh, ah = heads(r), heads(k), heads(v), heads(a)
    wh = decay.reshape(bsz, t, h, n)
    a_vec = -kk
    b_vec = kk * ah

    def step(S, inp):
        r_t, w_t, k_t, v_t, av_t, bv_t = inp
        sa = jnp.einsum('bhvk,bhk->bhv', S, av_t)
        S = (S * w_t[:, :, None, :] + sa[..., None] * bv_t[:, :, None, :]
             + v_t[..., None] * k_t[:, :, None, :])
        y_t = jnp.einsum('bhvk,bhk->bhv', S, r_t)
        return S, y_t

    seq_first = lambda z: jnp.swapaxes(z, 0, 1)
    S0 = jnp.zeros((bsz, h, n, n), jnp.float32)
    _, y = lax.scan(step, S0, (seq_first(rh), seq_first(wh), seq_first(kh),
                               seq_first(vh), seq_first(a_vec), seq_first(b_vec)))
    y = jnp.swapaxes(y, 0, 1)

    mean = jnp.mean(y, axis=-1, keepdims=True)
    var = jnp.mean(jnp.square(y - mean), axis=-1, keepdims=True)
    y = ((y - mean) * lax.rsqrt(var + GN_EPS)).reshape(bsz, t, d)
    y = y * gn_g.astype(jnp.float32) + gn_b.astype(jnp.float32)
    bonus = jnp.sum(rh * kh * r_k.astype(jnp.float32), axis=-1, keepdims=True) * vh
    y = (y + bonus.reshape(bsz, t, d)).astype(x.dtype)
    return (y * g) @ w_o


def conv_ffn(x, w_up, conv_w, conv_b, w_down):
    u = causal_dwconv(x @ w_up, conv_w) + conv_b
    gate, up = jnp.split(u, 2, axis=-1)
    return (jax.nn.silu(gate) * up) @ w_down


def setup_inputs(seed: int = 0) -> dict:
    key = jax.random.key(seed)
    ks = iter(jax.random.split(key, 64))
    D, F, H, N = D_MODEL, D_FF, N_HEADS, HEAD_SIZE
    NC, NR = N_CONV_LAYERS, N_RWKV_LAYERS

    def nrm(shape, fan_in, scale=1.0):
        return jax.random.normal(next(ks), shape, jnp.float32) * (scale * fan_in ** -0.5)

    def unif(shape, lo, hi):
        return jax.random.uniform(next(ks), shape, jnp.float32, lo, hi)

    def near(shape, center, s):
        return center + s * jax.random.normal(next(ks), shape, jnp.float32)

    return {
        "x": jax.random.normal(next(ks), (BATCH, SEQ, D), jnp.float32),
        "norm_g": near((DEPTH, 4, D), 1.0, 0.02),
        "sc_w_in": nrm((NC, D, 3 * D), D),
        "sc_conv": nrm((NC, CONV_WIDTH, D), CONV_WIDTH),
        "sc_w_out": nrm((NC, D, D), D),
        "rw_mu": unif((NR, 6, D), 0.0, 1.0),
        "rw_wr": nrm((NR, D, D), D),
        "rw_wk": nrm((NR, D, D), D),
        "rw_wv": nrm((NR, D, D), D),
        "rw_wo": nrm((NR, D, D), D),
        "rw_w0": unif((NR, D), -6.0, -1.0),
        "rw_w1": nrm((NR, D, D_DECAY_LORA), D),
        "rw_w2": nrm((NR, D_DECAY_LORA, D), D_DECAY_LORA, 0.1),
        "rw_a0": near((NR, D), 0.0, 0.1),
        "rw_a1": nrm((NR, D, D_AAA_LORA), D),
        "rw_a2": nrm((NR, D_AAA_LORA, D), D_AAA_LORA, 0.1),
        "rw_g1": nrm((NR, D, D_GATE_LORA), D),
        "rw_g2": nrm((NR, D_GATE_LORA, D), D_GATE_LORA),
        "rw_kk": near((NR, D), 0.85, 0.02),
        "rw_ka": near((NR, D), 1.0, 0.02),
        "rw_rk": near((NR, H, N), 0.0, 0.1),
        "rw_gn_g": near((NR, D), 1.0, 0.02),
        "rw_gn_b": near((NR, D), 0.0, 0.02),
        "ffn_w_up": nrm((DEPTH, D, 2 * F), D),
        "ffn_conv": nrm((DEPTH, CONV_WIDTH, 2 * F), CONV_WIDTH),
        "ffn_conv_b": near((DEPTH, 2 * F), 0.0, 0.02),
        "ffn_w_down": nrm((DEPTH, F, D), F),
    }


def reference(x, norm_g, sc_w_in, sc_conv, sc_w_out,
              rw_mu, rw_wr, rw_wk, rw_wv, rw_wo, rw_w0, rw_w1, rw_w2,
              rw_a0, rw_a1, rw_a2, rw_g1, rw_g2, rw_kk, rw_ka, rw_rk,
              rw_gn_g, rw_gn_b,
              ffn_w_up, ffn_conv, ffn_conv_b, ffn_w_down):
    for i in range(DEPTH):
        j = i // N_MIXERS
        h = rmsnorm(x, norm_g[i, 0])
        if i % N_MIXERS == 0:
            h = short_conv_mixer(h, sc_w_in[j], sc_conv[j], sc_w_out[j])
        else:
            h = rwkv7_time_mix(h, rw_mu[j], rw_wr[j], rw_wk[j], rw_wv[j], rw_wo[j],
                               rw_w0[j], rw_w1[j], rw_w2[j], rw_a0[j], rw_a1[j], rw_a2[j],
                               rw_g1[j], rw_g2[j], rw_kk[j], rw_ka[j], rw_rk[j],
                               rw_gn_g[j], rw_gn_b[j])
        x = x + rmsnorm(h, norm_g[i, 1])
        h = rmsnorm(x, norm_g[i, 2])
        h = conv_ffn(h, ffn_w_up[i], ffn_conv[i], ffn_conv_b[i], ffn_w_down[i])
        x = x + rmsnorm(h, norm_g[i, 3])
    return x
```

```python
import contextlib
import numpy as np
import concourse.bass as bass
import concourse.mybir as mybir
from concourse.bass_utils import run_bass_kernel_spmd

F32 = mybir.dt.float32
BF16 = mybir.dt.bfloat16
AF = mybir.ActivationFunctionType
ALU = mybir.AluOpType
ENGS = ("pe", "act", "dve", "pool", "sp")
BLK = {"pe": "tensor", "act": "scalar", "dve": "vector", "pool": "gpsimd", "sp": "sync"}
C0 = -float(np.exp(-0.5))
RMS_EPS = 1e-6
GN_EPS = 64e-5


class Prog:
    def __init__(self, nc, n_dma_sems=32, sync_same_engine=True):
        self.nc = nc
        self.nodes = []
        self.last_w = {}
        self.readers = {}
        self.n_dma_sems = n_dma_sems
        self.sync_same = sync_same_engine
        self.dma_nodes = []
        self.bar_deps = set()
        self.bar_seen = set()

    def barrier(self):
        last = {}
        for n in self.nodes:
            if not n["dma"]:
                last[n["eng"]] = n["id"]
        self.bar_deps = set(last.values()) | set(self.dma_nodes[-self.n_dma_sems:])
        self.bar_seen = set()
        self.last_w = {}
        self.readers = {}

    def op(self, eng, fn, reads=(), writes=(), dma=False):
        nid = len(self.nodes)
        deps = set()
        for k in reads:
            if k in self.last_w:
                deps.add(self.last_w[k])
        for k in writes:
            if k in self.last_w:
                deps.add(self.last_w[k])
            rd = self.readers.get(k)
            if rd:
                deps.update(rd[0].values())
                deps.update(rd[1])
        for k in reads:
            rd = self.readers.setdefault(k, ({}, []))
            if dma:
                rd[1].append(nid)
            else:
                rd[0][eng] = nid
        for k in writes:
            self.last_w[k] = nid
            self.readers[k] = ({}, [])
        if self.bar_deps and eng not in self.bar_seen:
            deps |= self.bar_deps
            self.bar_seen.add(eng)
        node = dict(eng=eng, fn=fn, deps=deps, dma=dma, id=nid)
        if dma:
            j = len(self.dma_nodes)
            node["slot"] = j % self.n_dma_sems
            node["val"] = 16 * (j // self.n_dma_sems + 1)
            if j >= self.n_dma_sems:
                deps.add(self.dma_nodes[j - self.n_dma_sems])
            self.dma_nodes.append(nid)
        deps.discard(nid)
        self.nodes.append(node)
        return nid

    def dma(self, eng, out, in_, reads=(), writes=(), **kw):
        return self.op(eng, lambda e: e.dma_start(out=out, in_=in_, **kw), reads, writes, dma=True)

    def _skip(self, dn, n):
        return (not dn["dma"]) and dn["eng"] == n["eng"] and (dn["eng"] == "pe" or not self.sync_same)

    def emit(self):
        nc = self.nc
        nodes = self.nodes
        needs = set()
        for n in nodes:
            for d in n["deps"]:
                dn = nodes[d]
                if dn["dma"] or self._skip(dn, n):
                    continue
                needs.add(d)
        cnt = {e: 0 for e in ENGS}
        for n in nodes:
            if not n["dma"] and n["id"] in needs:
                cnt[n["eng"]] += 1
                n["cval"] = cnt[n["eng"]]
        per_eng = {e: [n for n in nodes if n["eng"] == e] for e in ENGS}
        self.stats = {e: len(per_eng[e]) for e in ENGS}
        self.stats["incs"] = dict(cnt)
        self.stats["waits"] = {}
        with contextlib.ExitStack() as st:
            csem = {e: st.enter_context(nc.semaphore("c_" + e)) for e in ENGS}
            dsem = [st.enter_context(nc.semaphore("d_%d" % i)) for i in range(self.n_dma_sems)]
            block = st.enter_context(nc.Block())

            def make_body(e):
                def body(eng):
                    waited = {}
                    nw = 0
                    for n in per_eng[e]:
                        req = {}
                        for d in n["deps"]:
                            dn = nodes[d]
                            if dn["dma"]:
                                key, val = ("d", dn["slot"]), dn["val"]
                            else:
                                if self._skip(dn, n):
                                    continue
                                key, val = ("c", dn["eng"]), dn["cval"]
                            if val > req.get(key, 0):
                                req[key] = val
                        for key, val in req.items():
                            if waited.get(key, 0) >= val:
                                continue
                            waited[key] = val
                            eng.wait_ge(dsem[key[1]] if key[0] == "d" else csem[key[1]], val)
                            nw += 1
                        ins = n["fn"](eng)
                        if n["dma"]:
                            ins.then_inc(dsem[n["slot"]], 16)
                        elif "cval" in n:
                            ins.then_inc(csem[e], 1)
                    if e == "sp":
                        final = {}
                        for nid in self.dma_nodes:
                            dn = nodes[nid]
                            final[dn["slot"]] = max(final.get(dn["slot"], 0), dn["val"])
                        for slot, val in final.items():
                            eng.wait_ge(dsem[slot], val)
                        for e2 in ENGS:
                            if cnt[e2] > 0:
                                eng.wait_ge(csem[e2], cnt[e2])
                    self.stats["waits"][e] = nw
                return body

            for e in ENGS:
                if per_eng[e] or e == "sp":
                    getattr(block, BLK[e])(make_body(e))


class Cfg:
    def __init__(s, D=2048, F=5632, TOK=1024, GS=8, HALO=8, NT=3, LW=96, LA=96, LG=256, C=128):
        s.D, s.F, s.TOK, s.GS, s.HALO, s.NT, s.LW, s.LA, s.LG, s.C = D, F, TOK, GS, HALO, NT, LW, LA, LG, C
        s.KC = D // 128
        s.FC = F // 128
        s.T = TOK + HALO
        s.TT = s.T // NT
        assert s.TT * NT == s.T and s.TT <= 512
        s.SO = HALO - 2
        s.TS = s.T - s.SO
        s.NCS = -(-s.TS // C)
        s.TSP = s.NCS * C
        s.TL = s.SO + s.TSP
        s.NP = s.KC
        b = {}
        n = 0
        KC, FC = s.KC, s.FC
        for i in range(2):
            for j in range(4):
                b["ng%d%d" % (i, j)] = n; n += KC
        for t in range(3):
            b["scw%d" % t] = n; n += KC
        for p in range(6):
            b["mu%d" % p] = n; n += KC
        for nm in ("w0", "a0", "kkw", "kaw", "rkw", "gng", "gnb"):
            b[nm] = n; n += KC
        for i in range(2):
            for t in range(3):
                b["fcw%d%d" % (i, t)] = n; n += 2 * FC
            b["fcb%d" % i] = n; n += 2 * FC
        s.pvb = b
        s.NPV = n
        s.cb = dict(ident=0, blk=128, mask4=256, maskt=768, ai=1024, ablk=1152, rm=1280)
        s.NCONST = 1280 + s.TSP


def _colmajor(v):
    v = np.asarray(v, np.float32).reshape(-1, 128)
    return np.ascontiguousarray(v.T)


def pack_pv(c, inp):
    pv = np.zeros((128, c.NPV), np.float32)

    def put(name, v):
        cm = _colmajor(v)
        pv[:, c.pvb[name]:c.pvb[name] + cm.shape[1]] = cm
    for i in range(2):
        for j in range(4):
            put("ng%d%d" % (i, j), inp["norm_g"][i, j])
    for t in range(3):
        put("scw%d" % t, inp["sc_conv"][0, t])
    for p in range(6):
        put("mu%d" % p, inp["rw_mu"][0, p])
    put("w0", inp["rw_w0"][0]); put("a0", inp["rw_a0"][0]); put("kkw", inp["rw_kk"][0])
    put("kaw", inp["rw_ka"][0]); put("rkw", inp["rw_rk"][0].reshape(-1))
    put("gng", inp["rw_gn_g"][0]); put("gnb", inp["rw_gn_b"][0])
    for i in range(2):
        for t in range(3):
            put("fcw%d%d" % (i, t), inp["ffn_conv"][i, t])
        put("fcb%d" % i, inp["ffn_conv_b"][i])
    return pv


def make_consts(c):
    k = np.zeros((128, c.NCONST), np.float32)
    cb = c.cb
    k[:, cb["ident"]:cb["ident"] + 128] = np.eye(128)
    blk = np.zeros((128, 128), np.float32)
    blk[:64, :64] = 1
    blk[64:, 64:] = 1
    k[:, cb["blk"]:cb["blk"] + 128] = blk
    j = np.arange(128)
    strict = (j[:, None] < j[None, :]).astype(np.float32)
    incl = (j[:, None] <= j[None, :]).astype(np.float32)
    k[:, cb["mask4"]:cb["mask4"] + 512] = np.concatenate([strict, incl, strict, incl], axis=1)
    k[:, cb["maskt"]:cb["maskt"] + 256] = np.concatenate([strict.T, strict.T], axis=1)
    ai = np.zeros((128, 128), np.float32)
    ai[j, (j + 64) % 128] = 1
    k[:, cb["ai"]:cb["ai"] + 128] = ai
    k[:, cb["ablk"]:cb["ablk"] + 128] = 1.0 - blk
    rm = np.ones((c.TSP,), np.float32)
    rm[::c.C] = 0
    k[:, cb["rm"]:cb["rm"] + c.TSP] = rm[None, :]
    return k


class Builder:
    def __init__(self, cfg, mode, stop_after=None):
        self.c = cfg
        self.mode = mode
        self.stop_after = stop_after
        self.nc = bass.Bass("TRN2", target_bir_lowering=False)
        self.P = Prog(self.nc)
        self.uid = 0
        self.bigc = 0
        self.smallc = 0
        self.wnext = 0
        self.ostc = 0
        self.dram = {}

    def sb(self, st, shape, dt, name="t"):
        self.uid += 1
        return st.enter_context(self.nc.sbuf_tensor("%s_%d" % (name, self.uid), list(shape), dt))

    def dt_(self, name, shape, dt, kind):
        t = self.nc.dram_tensor(name, list(shape), dt, kind=kind).ap()
        self.dram[name] = t
        return t

    def handoff(self, name, shape, dt, produced_in):
        if self.mode == "fused":
            kind = "Internal"
        elif self.mode == produced_in:
            kind = "ExternalOutput"
        elif self.mode == "A" and produced_in == "B":
            return None
        else:
            kind = "ExternalInput"
        return self.dt_(name, shape, dt, kind)

    def set_w(self, st, size, n=3):
        self.wsize = size
        self.wslots = [self.sb(st, [128, size], BF16, "w") for _ in range(n)]
        self.wnext = 0

    @contextlib.contextmanager
    def scope(self):
        with contextlib.ExitStack() as st:
            yield st
        self.P.barrier()

    def big_ps(self):
        i = self.bigc % 6
        self.bigc += 1
        return self.ps[i], ("ps", i)

    def small_ps(self):
        i = 6 + self.smallc % 2
        self.smallc += 1
        return self.ps[i], ("ps", i)

    def pv(self, name, col=0):
        b = self.c.pvb[name] + col
        return self.PV[:, b:b + 1]

    def A(self, out, in_, func, reads, writes, scale=None, bias=None):
        kw = {}
        if scale is not None:
            kw["scale"] = scale
        if bias is not None:
            kw["bias"] = bias
        self.P.op("act", lambda e: e.activation(out=out, in_=in_, func=func, **kw), reads, writes)

    def TT(self, eng, out, in0, in1, op, reads, writes):
        self.P.op(eng, lambda e: e.tensor_tensor(out=out, in0=in0, in1=in1, op=op), reads, writes)

    def TS(self, eng, out, in0, s1, op0, reads, writes, s2=None, op1=None):
        if op1 is None:
            self.P.op(eng, lambda e: e.tensor_scalar(out=out, in0=in0, scalar1=s1, scalar2=None, op0=op0), reads, writes)
        else:
            self.P.op(eng, lambda e: e.tensor_scalar(out=out, in0=in0, scalar1=s1, scalar2=s2, op0=op0, op1=op1), reads, writes)

    def STT(self, out, in0, scalar, in1, op0, op1, reads, writes):
        self.P.op("dve", lambda e: e.scalar_tensor_tensor(out=out, in0=in0, scalar=scalar, in1=in1, op0=op0, op1=op1), reads, writes)

    def MM(self, out, lhsT, rhs, reads, writes, start=True, stop=True):
        self.P.op("pe", lambda e: e.matmul(out, lhsT=lhsT, rhs=rhs, start=start, stop=stop), reads, writes)

    def CP(self, eng, out, in_, reads, writes):
        if eng == "act":
            self.A(out, in_, AF.Copy, reads, writes)
        else:
            self.P.op(eng, lambda e: e.tensor_copy(out=out, in_=in_), reads, writes)

    def MS(self, eng, ap, val, writes):
        self.P.op(eng, lambda e: e.memset(ap, val), (), writes)

    def linear(self, rhs_fn, nk, kp, blocks, epilogue):
        c, P = self.c, self.P
        nb = len(blocks)
        slots = {}
        fresh = set()
        MAXC = 4

        def issue(bi):
            srcs, widths = blocks[bi]
            s = self.wnext
            self.wnext = (self.wnext + 1) % len(self.wslots)
            nch = len(widths)
            wmax = max(widths)
            assert nch <= MAXC and nk * nch * wmax <= self.wsize, (nk, nch, wmax, self.wsize)
            flat = self.wslots[s]
            allk = [("w", s, ci) for ci in range(MAXC)]
            if isinstance(srcs, list):
                for ci, src in enumerate(srcs):
                    dst = flat[0:kp, ci * nk * wmax:(ci + 1) * nk * wmax].rearrange("p (k n) -> p k n", k=nk)[:, :, 0:widths[ci]]
                    wk = allk if (s not in fresh) else [("w", s, ci)]
                    fresh.add(s)
                    P.dma("pool", dst, src, writes=wk)
                lhs = lambda kc, ci, wd: flat[0:kp, ci * nk * wmax + kc * wmax: ci * nk * wmax + kc * wmax + wd]
            else:
                dst = flat[0:kp, 0:nk * nch * wmax].rearrange("p (k n) -> p k n", k=nk)
                fresh.add(s)
                P.dma("pool", dst, srcs, writes=allk)
                lhs = lambda kc, ci, wd: flat[0:kp, kc * nch * wmax + ci * wmax: kc * nch * wmax + ci * wmax + wd]
            slots[bi] = (s, lhs)

        ahead = len(self.wslots) - 1
        for bi in range(min(ahead, nb)):
            issue(bi)
        for bi in range(nb):
            if bi + ahead < nb:
                issue(bi + ahead)
            s, lhs = slots[bi]
            widths = blocks[bi][1]
            for ci, wd in enumerate(widths):
                pss = []
                for tt in range(c.NT):
                    ps, pk = self.big_ps()
                    for kc in range(nk):
                        rhs, rkeys = rhs_fn(kc, tt)
                        self.MM(ps[0:wd, 0:c.TT], lhs(kc, ci, wd), rhs, [("w", s, ci)] + list(rkeys), [pk],
                                start=(kc == 0), stop=(kc == nk - 1))
                    pss.append((ps, pk))
                epilogue(bi, ci, pss)

    def tts(self, tt):
        return slice(tt * self.c.TT, (tt + 1) * self.c.TT)

    def stats(self, st, M):
        c = self.c
        RSTD = self.sb(st, [128, c.T], F32, "rstd")
        SQ = [self.sb(st, [128, c.TT], F32, "sq") for _ in range(2)]
        for tt in range(c.NT):
            ps, pk = self.small_ps()
            for kc in range(c.KC):
                sq = SQ[kc % 2]
                self.A(sq[:, :], M[:, kc, self.tts(tt)], AF.Square, [("M", kc)], [("sq", kc % 2)])
                self.MM(ps[:, 0:c.TT], self.ONES[:, :], sq[:, :], [("sq", kc % 2), "ones"], [pk],
                        start=(kc == 0), stop=(kc == c.KC - 1))
            self.A(RSTD[:, self.tts(tt)], ps[:, 0:c.TT], AF.Sqrt, [pk, "epsr"], [("rstd", tt)], scale=1.0 / c.D, bias=self.EPSR[:, 0:1])
            self.P.op("dve", lambda e, tt=tt: e.reciprocal(out=RSTD[:, self.tts(tt)], in_=RSTD[:, self.tts(tt)]),
                      [("rstd", tt)], [("rstd", tt)])
        return RSTD

    def norm(self, M, H, gname):
        c = self.c
        with self.scope() as st:
            RSTD = self.stats(st, M)
            for kc in range(c.KC):
                for tt in range(c.NT):
                    self.STT(H[:, kc, self.tts(tt)], M[:, kc, self.tts(tt)], self.pv(gname, kc), RSTD[:, self.tts(tt)],
                             ALU.mult, ALU.mult, [("M", kc), ("rstd", tt), "pv"], [("H", kc)])

    def residual(self, M, src, dst, gname, skey, dkey, final=False):
        c = self.c
        with self.scope() as st:
            RSTD = self.stats(st, M)
            XIN = [self.sb(st, [128, c.T], F32, "xin") for _ in range(2)]
            for kc in range(c.KC):
                xin = XIN[kc % 2]
                self.P.dma("sp", xin[:, :], src[kc * 128:(kc + 1) * 128, :], reads=[(skey, kc)], writes=[("xin", kc % 2)])
                for tt in range(c.NT):
                    self.STT(M[:, kc, self.tts(tt)], M[:, kc, self.tts(tt)], self.pv(gname, kc), RSTD[:, self.tts(tt)],
                             ALU.mult, ALU.mult, [("M", kc), ("rstd", tt), "pv"], [("M", kc)])
                self.TT("pool", M[:, kc, :], M[:, kc, :], xin[:, :], ALU.add, [("M", kc), ("xin", kc % 2)], [("M", kc)])
                if final:
                    self.P.dma("sp", dst[kc * 128:(kc + 1) * 128, :], M[:, kc, c.HALO:c.T], reads=[("M", kc)], writes=[(dkey, kc)])
                else:
                    self.P.dma("sp", dst[kc * 128:(kc + 1) * 128, :], M[:, kc, :], reads=[("M", kc)], writes=[(dkey, kc)])

    def sconv(self, st, H, M, W_in, W_out):
        c, P = self.c, self.P
        KC, T = c.KC, c.T
        G = self.sb(st, [128, KC, T], BF16, "G")
        self.set_w(st, 6144)
        TB = [self.sb(st, [128, T], F32, "tb")] * 2
        TC = [self.sb(st, [128, T], F32, "tc")] * 2
        CH = [self.sb(st, [128, T], F32, "ch")] * 2
        CV = [self.sb(st, [128, T], F32, "cv")] * 2
        for kc in range(KC):
            self.MS("pool", G[:, kc, 0:2], 0.0, [("G", kc)])
        w5 = W_in.rearrange("(k p) (c j n) -> p k c j n", p=128, c=3, j=KC)
        blocks = [([w5[:, :, ci, j, :] for ci in range(3)], [128, 128, 128]) for j in range(KC)]

        def rhs_fn(kc, tt):
            return H[:, kc, self.tts(tt)], [("H", kc)]

        def epi(j, ci, pss):
            b = 0
            for tt, (ps, pk) in enumerate(pss):
                sl = self.tts(tt)
                if ci == 0:
                    self.CP("act", TB[b][:, sl], ps[:, 0:c.TT], [pk], [("tb", b)])
                elif ci == 1:
                    self.CP("act", TC[b][:, sl], ps[:, 0:c.TT], [pk], [("tc", b)])
                else:
                    self.TT("dve", CH[b][:, sl], ps[:, 0:c.TT], TC[b][:, sl], ALU.mult, [pk, ("tc", b)], [("ch", b)])
            if ci == 2:
                self.TS("dve", CH[b][:, 0:c.HALO], CH[b][:, 0:c.HALO], self.CM[:, 0:1], ALU.mult, [("ch", b), "cm"], [("ch", b)])
                self.A(CV[b][:, 2:T], CH[b][:, 2:T], AF.Copy, [("ch", b), "pv"], [("cv", b)], scale=self.pv("scw2", j))
                self.STT(CV[b][:, 2:T], CH[b][:, 1:T - 1], self.pv("scw1", j), CV[b][:, 2:T], ALU.mult, ALU.add,
                         [("ch", b), ("cv", b), "pv"], [("cv", b)])
                self.STT(CV[b][:, 2:T], CH[b][:, 0:T - 2], self.pv("scw0", j), CV[b][:, 2:T], ALU.mult, ALU.add,
                         [("ch", b), ("cv", b), "pv"], [("cv", b)])
                self.TT("pool", G[:, j, 2:T], TB[b][:, 2:T], CV[b][:, 2:T], ALU.mult, [("tb", b), ("cv", b)], [("G", j)])

        self.linear(rhs_fn, KC, 128, blocks, epi)
        nb = min(2, KC)
        blocks = [(W_out[:, o * 128:(o + nb) * 128].rearrange("(k p) n -> p k n", p=128), [128] * nb) for o in range(0, KC, nb)]

        def rhs2(kc, tt):
            return G[:, kc, self.tts(tt)], [("G", kc)]

        def epi2(bi, ci, pss):
            oc = bi * nb + ci
            for tt, (ps, pk) in enumerate(pss):
                self.CP("act", M[:, oc, self.tts(tt)], ps[:, 0:c.TT], [pk], [("M", oc)])

        self.linear(rhs2, KC, 128, blocks, epi2)

    def ffn(self, st, li, H, M, W_up, W_dn):
        c, P = self.c, self.P
        KC, FC, T, GS = c.KC, c.FC, c.T, c.GS
        self.set_w(st, 4096)
        ACTB = self.sb(st, [128, 2 * GS, T], BF16, "actb")
        UG = [self.sb(st, [128, T], F32, "ug") for _ in range(2)]
        UU = [self.sb(st, [128, T], F32, "uu") for _ in range(2)]
        CG = self.sb(st, [128, T], F32, "cg")
        CU = self.sb(st, [128, T], F32, "cu")
        SL = CG
        for s in range(2 * GS):
            self.MS("pool", ACTB[:, s, 0:2], 0.0, [("actb", s)])
        w5 = W_up.rearrange("(k p) (c f n) -> p k c f n", p=128, c=2, f=FC)
        groups = [list(range(g, min(g + GS, FC))) for g in range(0, FC, GS)]
        fw = lambda t, col: self.pv("fcw%d%d" % (li, t), col)
        fb = lambda col: self.pv("fcb%d" % li, col)

        def rhs_fn(kc, tt):
            return H[:, kc, self.tts(tt)], [("H", kc)]

        for gi, grp in enumerate(groups):
            sbase = (gi % 2) * GS
            blocks = [([w5[:, :, ci, f, :] for ci in range(2)], [128, 128]) for f in grp]

            def epi(bi, ci, pss, grp=grp, sbase=sbase):
                f = grp[bi]
                b = f % 2
                U = UG if ci == 0 else UU
                nm = "ug" if ci == 0 else "uu"
                for tt, (ps, pk) in enumerate(pss):
                    self.CP("act", U[b][:, self.tts(tt)], ps[:, 0:c.TT], [pk], [(nm, b)])
                self.TS("dve", U[b][:, 0:c.HALO], U[b][:, 0:c.HALO], self.CM[:, 0:1], ALU.mult, [(nm, b), "cm"], [(nm, b)])
                if ci == 1:
                    for (U2, nm2, CVt, cn, col) in ((UG, "ug", CG, "cg", f), (UU, "uu", CU, "cu", FC + f)):
                        self.TS("pool", CVt[:, 2:T], U2[b][:, 2:T], fw(2, col), ALU.mult, [(nm2, b), "pv"], [cn], s2=fb(col), op1=ALU.add)
                        self.STT(CVt[:, 2:T], U2[b][:, 1:T - 1], fw(1, col), CVt[:, 2:T], ALU.mult, ALU.add, [(nm2, b), cn, "pv"], [cn])
                        self.STT(CVt[:, 2:T], U2[b][:, 0:T - 2], fw(0, col), CVt[:, 2:T], ALU.mult, ALU.add, [(nm2, b), cn, "pv"], [cn])
                    self.A(SL[:, 2:T], CG[:, 2:T], AF.Silu, ["cg"], ["cg"])
                    self.TT("pool", ACTB[:, sbase + bi, 2:T], SL[:, 2:T], CU[:, 2:T], ALU.mult, ["cg", "cu"], [("actb", sbase + bi)])

            self.linear(rhs_fn, KC, 128, blocks, epi)
            ng = len(grp)
            nb = min(4, KC)
            r0 = grp[0] * 128
            blocks = [(W_dn[r0:r0 + ng * 128, o * 128:(o + nb) * 128].rearrange("(k p) n -> p k n", p=128), [128] * nb)
                      for o in range(0, KC, nb)]

            def rhs2(kc, tt, sbase=sbase):
                return ACTB[:, sbase + kc, self.tts(tt)], [("actb", sbase + kc)]

            def epi2(bi, ci, pss, gi=gi, nb=nb):
                oc = bi * nb + ci
                for tt, (ps, pk) in enumerate(pss):
                    sl = self.tts(tt)
                    if gi == 0:
                        self.CP("act", M[:, oc, sl], ps[:, 0:c.TT], [pk], [("M", oc)])
                    else:
                        self.TT("dve", M[:, oc, sl], ps[:, 0:c.TT], M[:, oc, sl], ALU.add, [pk, ("M", oc)], [("M", oc)])

            self.linear(rhs2, ng, 128, blocks, epi2)

    def ost_epi(self, OST, dst, dkey, func=None, bias_name=None, eng="act"):
        c = self.c

        def epi(oc, pss):
            i = self.ostc % len(OST)
            self.ostc += 1
            o = OST[i]
            for tt, (ps, pk) in enumerate(pss):
                sl = self.tts(tt)
                if func is not None:
                    self.A(o[:, sl], ps[:, 0:c.TT], func, [pk, "pv"], [("ost", i)], bias=self.pv(bias_name, oc))
                else:
                    self.CP(eng, o[:, sl], ps[:, 0:c.TT], [pk], [("ost", i)])
            self.P.dma("sp", dst[oc * 128:(oc + 1) * 128, :], o[:, :], reads=[("ost", i)], writes=[(dkey, oc)])
        return epi

    def rwkv_proj(self, st, H, w):
        c, P = self.c, self.P
        KC, T = c.KC, c.T
        XX = self.sb(st, [128, KC, T], BF16, "xx")
        XP = [self.sb(st, [128, KC, T], BF16, "xp") for _ in range(2)]
        T1 = self.sb(st, [128, T], BF16, "t1")
        T2 = self.sb(st, [128, T], BF16, "t2")
        T3 = self.sb(st, [128, 2, T], BF16, "t3")
        OST = [self.sb(st, [128, T], F32, "ost") for _ in range(3)]
        hk = [("H", kc) for kc in range(KC)]
        self.TS("pool", H[:, :, 0:c.HALO], H[:, :, 0:c.HALO], self.CM[:, 0:1], ALU.mult, hk + ["cm"], hk)
        for kc in range(KC):
            self.MS("pool", XX[:, kc, 0:1], 0.0, [("xx", kc)])
            self.TT("pool", XX[:, kc, 1:T], H[:, kc, 0:T - 1], H[:, kc, 1:T], ALU.subtract, [("H", kc)], [("xx", kc)])
        mixc = [0]

        def mix(p):
            b = mixc[0] % 2
            mixc[0] += 1
            for kc in range(KC):
                self.STT(XP[b][:, kc, :], XX[:, kc, :], self.pv("mu%d" % p, kc), H[:, kc, :], ALU.mult, ALU.add,
                         [("xx", kc), ("H", kc), "pv"], [("xp", b, kc)])
            return lambda kc, tt: (XP[b][:, kc, self.tts(tt)], [("xp", b, kc)])

        def lora_in(W1, width, dst_fn, func, p):
            rhs = mix(p)
            nch = -(-width // 128)
            blocks = [(W1.rearrange("(k p) n -> p k n", p=128), [128] * nch if nch > 1 else [width])]

            def epi(bi, ci, pss):
                wd = min(128, width - ci * 128)
                for tt, (ps, pk) in enumerate(pss):
                    self.A(dst_fn(ci)[0:wd, self.tts(tt)], ps[0:wd, 0:c.TT], func, [pk], [("lora", p, ci)])
            self.linear(rhs, KC, 128, blocks, epi)

        def lora_out(W2, width, src_fn, p, epi_oc):
            nk = -(-width // 128)
            kp = width if nk == 1 else 128
            nb = min(4, KC)
            blocks = [(W2[:, o * 128:(o + nb) * 128].rearrange("(k p) n -> p k n", p=kp), [128] * nb) for o in range(0, KC, nb)]

            def rhs(kc, tt):
                return src_fn(kc)[0:kp, self.tts(tt)], [("lora", p, kc)]
            self.linear(rhs, nk, kp, blocks, lambda bi, ci, pss: epi_oc(bi * nb + ci, pss))

        lora_in(w["rw_w1"], c.LW, lambda ci: T1, AF.Tanh, 1)
        lora_out(w["rw_w2"], c.LW, lambda kc: T1, 1, self.ost_epi(OST, self.d["sg"], "d_sg", AF.Sigmoid, "w0"))
        lora_in(w["rw_a1"], c.LA, lambda ci: T2, AF.Copy, 4)
        lora_out(w["rw_a2"], c.LA, lambda kc: T2, 4, self.ost_epi(OST, self.d["a"], "d_a", AF.Sigmoid, "a0"))
        lora_in(w["rw_g1"], c.LG, lambda ci: T3[:, ci, :], AF.Sigmoid, 5)
        lora_out(w["rw_g2"], c.LG, lambda kc: T3[:, kc, :], 5, self.ost_epi(OST, self.d["g"], "d_g"))
        nb = min(2, KC)
        for (p, wn, dn) in ((0, "rw_wr", "r"), (2, "rw_wk", "k"), (3, "rw_wv", "v")):
            rhs = mix(p)
            W = w[wn]
            blocks = [(W[:, o * 128:(o + nb) * 128].rearrange("(k p) n -> p k n", p=128), [128] * nb) for o in range(0, KC, nb)]
            ep = self.ost_epi(OST, self.d[dn], "d_" + dn)
            self.linear(rhs, KC, 128, blocks, lambda bi, ci, pss, ep=ep: ep(bi * nb + ci, pss))

    def rwkv_scan(self, st):
        c, P = self.c, self.P
        T, TL, SO, TSP, NCS, C, KC = c.T, c.TL, c.SO, c.TSP, c.NCS, c.C, c.KC
        cb = c.cb
        CF = self.CF
        names = ("r", "k", "v", "sg", "a")
        IN = {n: [self.sb(st, [128, TL], F32, "in_" + n) for _ in range(2)] for n in names}
        for n in names:
            for b in range(2):
                self.MS("pool", IN[n][b][:, T:TL], 0.0, [("in", n, b)])
        tmpn = ("cs", "ec", "enc", "ecp", "kk", "sq", "k2", "bv", "rkk")
        TM_ = {n: self.sb(st, [128, TSP], F32, "p_" + n) for n in tmpn}
        SC = [self.sb(st, [128, NCS, 4 * C], BF16, "sc") for _ in range(2)]
        V = [self.sb(st, [128, TSP], BF16, "v") for _ in range(2)]
        PC = [self.sb(st, [128, NCS], F32, "pc") for _ in range(2)]
        Y = [self.sb(st, [128, TSP], F32, "y") for _ in range(2)]
        ZT = [self.sb(st, [128, TSP], BF16, "zt") for _ in range(2)]
        BON = [self.sb(st, [128, TSP], F32, "bon") for _ in range(2)]
        TMt = [self.sb(st, [128, 4, 128], BF16, "tm") for _ in range(2)]
        AM = [self.sb(st, [128, 2, 512], BF16, "am") for _ in range(2)]
        Y0 = [self.sb(st, [128, 2, 128], BF16, "y0") for _ in range(2)]
        XY = [self.sb(st, [128, 4, 128], BF16, "xy") for _ in range(2)]
        ZZ = [self.sb(st, [128, 2, 128], BF16, "zz") for _ in range(2)]
        LBD = self.sb(st, [128, 128], F32, "lbd")
        QP = self.sb(st, [128, 128], F32, "qp")
        STt = [self.sb(st, [128, 128], F32, "st") for _ in range(2)]
        SEND = self.sb(st, [128, c.NP, 128], F32, "send")
        mask4 = CF[:, cb["mask4"]:cb["mask4"] + 512]
        maskt = CF[:, cb["maskt"]:cb["maskt"] + 256]
        blk = CF[:, cb["blk"]:cb["blk"] + 128]
        identf = CF[:, cb["ident"]:cb["ident"] + 128]
        rm = CF[:, cb["rm"]:cb["rm"] + TSP]
        IDB = self.IDB
        v3 = lambda ap: ap.rearrange("p (c n) -> p c n", n=C)

        for hp in range(c.NP):
            b = hp % 2
            I = {n: IN[n][b] for n in names}
            for n in names:
                P.dma("sp", I[n][:, 0:T], self.d[n][hp * 128:(hp + 1) * 128, :], reads=[("d_" + n, hp)], writes=[("in", n, b)])
            for n in ("k", "v", "sg"):
                self.TS("pool", I[n][:, 0:c.HALO], I[n][:, 0:c.HALO], self.CM[:, 0:1], ALU.mult, [("in", n, b), "cm"], [("in", n, b)])
            vs = lambda n: I[n][:, SO:TL]
            rk = lambda n: [("in", n, b)]
            t = TM_
            P.op("dve", lambda e, I=I: e.tensor_tensor_scan(out=t["cs"][:, :], data0=rm, data1=I["sg"][:, SO:TL], initial=0.0,
                                                        op0=ALU.mult, op1=ALU.add), rk("sg") + ["cf"], ["cs"])
            self.A(t["ec"][:, :], t["cs"][:, :], AF.Exp, ["cs"], ["ec"], scale=C0)
            self.A(t["enc"][:, :], t["cs"][:, :], AF.Exp, ["cs"], ["enc"], scale=-C0)
            self.TT("pool", t["ecp"][:, :], t["cs"][:, :], vs("sg"), ALU.subtract, ["cs"] + rk("sg"), ["ecp"])
            self.A(t["ecp"][:, :], t["ecp"][:, :], AF.Exp, ["ecp"], ["ecp"], scale=C0)
            self.CP("dve", PC[b][:, :], v3(t["ec"][:, :])[:, :, C - 1], ["ec"], [("pc", b)])
            self.A(t["kk"][:, :], vs("k"), AF.Copy, rk("k") + ["pv"], ["kk"], scale=self.pv("kkw", hp))
            self.A(t["sq"][:, :], t["kk"][:, :], AF.Square, ["kk"], ["sq"])
            for o in range(0, TSP, 512):
                n_ = min(512, TSP - o)
                ps, pk = self.small_ps()
                self.MM(ps[:, 0:n_], blk, t["sq"][:, o:o + n_], ["sq", "cf"], [pk])
                self.A(t["rkk"][:, o:o + n_], ps[:, 0:n_], AF.Sqrt, [pk], ["rkk"])
            self.TS("dve", t["rkk"][:, :], t["rkk"][:, :], 1e-12, ALU.max, ["rkk"], ["rkk"])
            P.op("dve", lambda e: e.reciprocal(out=t["rkk"][:, :], in_=t["rkk"][:, :]), ["rkk"], ["rkk"])
            self.TT("pool", t["kk"][:, :], t["kk"][:, :], t["rkk"][:, :], ALU.mult, ["kk", "rkk"], ["kk"])
            self.TS("dve", t["k2"][:, :], vs("a"), 1.0, ALU.subtract, rk("a") + ["pv"], ["k2"], s2=self.pv("kaw", hp), op1=ALU.mult)
            self.STT(t["k2"][:, :], t["k2"][:, :], 1.0, vs("k"), ALU.add, ALU.mult, ["k2"] + rk("k"), ["k2"])
            self.TT("pool", t["bv"][:, :], t["kk"][:, :], vs("a"), ALU.mult, ["kk"] + rk("a"), ["bv"])
            sc = SC[b]
            self.STT(sc[:, :, 0:C], v3(t["kk"][:, :]), -1.0, v3(t["ecp"][:, :]), ALU.mult, ALU.mult, ["kk", "ecp"], [("sc", b)])
            self.TT("pool", sc[:, :, C:2 * C], v3(vs("r")), v3(t["ec"][:, :]), ALU.mult, rk("r") + ["ec"], [("sc", b)])
            self.TT("pool", sc[:, :, 2 * C:3 * C], v3(t["bv"][:, :]), v3(t["enc"][:, :]), ALU.mult, ["bv", "enc"], [("sc", b)])
            self.TT("dve", sc[:, :, 3 * C:4 * C], v3(t["k2"][:, :]), v3(t["enc"][:, :]), ALU.mult, ["k2", "enc"], [("sc", b)])
            self.CP("act", V[b][:, :], vs("v"), rk("v"), [("v", b)])
            self.STT(t["sq"][:, :], vs("r"), self.pv("rkw", hp), t["k2"][:, :], ALU.mult, ALU.mult, rk("r") + ["k2", "pv"], ["sq"])
            for o in range(0, TSP, 512):
                n_ = min(512, TSP - o)
                ps, pk = self.small_ps()
                self.MM(ps[:, 0:n_], blk, t["sq"][:, o:o + n_], ["sq", "cf"], [pk])
                self.TT("dve", BON[b][:, o:o + n_], ps[:, 0:n_], I["v"][:, SO + o:SO + o + n_], ALU.mult, [pk] + rk("v"), [("bon", b)])
            P.dma("sp", self.d["bon"][hp * 128:(hp + 1) * 128, :], BON[b][:, :], reads=[("bon", b)], writes=[("d_bon", hp)])

            self.CP("pool", STt[0][:, :], CF[:, cb["ai"]:cb["ai"] + 128], ["cf"], [("st", 0)])
            sck, vk = ("sc", b), ("v", b)
            import os
            LIM = int(os.environ.get('SCAN_STEPS', '99'))
            for ch in range(NCS if LIM > 0 else 0):
                par = ch % 2
                tm, am, y0 = TMt[par], AM[par], Y0[par]
                cs_ = lambda w0, w1=None: sc[:, ch, w0 * C:(w1 if w1 else w0 + 1) * C]
                pprev = self.ONE1[:, 0:1] if ch == 0 else PC[b][:, ch - 1:ch]
                ppk = ["one1"] if ch == 0 else [("pc", b)]
                stp, stn = STt[ch % 2], STt[(ch + 1) % 2]
                spk, snk = ("st", ch % 2), ("st", (ch + 1) % 2)
                psT, kT = self.big_ps()
                for wi, src in enumerate((cs_(0), cs_(2), cs_(3), V[b][:, ch * C:(ch + 1) * C])):
                    self.MM(psT[:, wi * 128:(wi + 1) * 128], src, IDB[:, :], [sck if wi < 3 else vk, "idb"], [kT])
                self.CP("act", tm[:, :, :], psT[:, 0:512].rearrange("p (w n) -> p w n", w=4), [kT], [("tm", par)])
                if LIM <= 1:
                    continue
                for e in range(2):
                    pe = slice(e * 64, (e + 1) * 64)
                    psG, kG = self.big_ps()
                    self.MM(psG[:, 0:256], sc[pe, ch, 2 * C:3 * C], sc[pe, ch, 0:2 * C], [sck], [kG])
                    self.MM(psG[:, 256:512], sc[pe, ch, 3 * C:4 * C], sc[pe, ch, 0:2 * C], [sck], [kG])
                    self.TT("dve", am[:, e, :], psG[:, 0:512], mask4, ALU.mult, [kG, "cf"], [("am", par, e)])
                for e in range(2):
                    pe = slice(e * 64, (e + 1) * 64)
                    psY, kY = self.big_ps()
                    self.MM(psY[:, 0:128], sc[pe, ch, 0:C], sc[pe, ch, 2 * C:3 * C], [sck], [kY])
                    self.TT("dve", y0[:, e, :], psY[:, 0:128], maskt[:, 0:128], ALU.mult, [kY, "cf"], [("y0", par, e)])
                if LIM <= 2:
                    continue
                z = ZZ[0]
                zk = ("zz", 0)
                psV, kV = self.big_ps()
                for e in range(2):
                    self.MM(psV[:, e * 64:(e + 1) * 64], am[:, e, 256:384], tm[:, 3, e * 64:(e + 1) * 64], [("am", par, e), ("tm", par)], [kV])
                self.CP("pool", z[:, :, 0:64], tm[:, 0, :].rearrange("p (e n) -> p e n", e=2), [("tm", par)], [zk])
                self.CP("act", z[:, :, 64:128], psV[:, 0:128].rearrange("p (e n) -> p e n", e=2), [kV], [zk])
                if LIM <= 3:
                    continue
                NL = 7
                X = [am[:, e, 0:128] for e in range(2)]
                Yv = [y0[:, e, :] for e in range(2)]
                xk = [("am", par, 0), ("am", par, 1)]
                yk = [("y0", par, 0), ("y0", par, 1)]
                zi = 0
                for l in range(NL):
                    zc, zn = ZZ[zi], ZZ[1 - zi]
                    zck, znk = ("zz", zi), ("zz", 1 - zi)
                    psZ, kZ = self.big_ps()
                    for e in range(2):
                        self.MM(psZ[:, e * 128:(e + 1) * 128], X[e], zc[:, e, :], [xk[e], zck], [kZ])
                    self.TT("dve", zn[:, :, :], psZ[:, 0:256].rearrange("p (e n) -> p e n", e=2), zc[:, :, :], ALU.add, [kZ, zck], [znk])
                    zi = 1 - zi
                    if l < NL - 1:
                        xy = XY[l % 2]
                        xyk = ("xy", l % 2)
                        psX, kX = self.big_ps()
                        ny = 2 if l < NL - 2 else 0
                        for e in range(2):
                            self.MM(psX[:, e * 128:(e + 1) * 128], Yv[e], X[e], [xk[e], yk[e]], [kX])
                        for e in range(ny):
                            self.MM(psX[:, (2 + e) * 128:(3 + e) * 128], X[e], Yv[e], [xk[e], yk[e]], [kX])
                        nn = 2 + ny
                        self.CP("act", xy[:, 0:nn, :], psX[:, 0:nn * 128].rearrange("p (w n) -> p w n", w=nn), [kX], [xyk])
                        X = [xy[:, e, :] for e in range(2)]
                        Yv = [xy[:, 2 + e, :] for e in range(2)]
                        xk = [xyk, xyk]
                        yk = [xyk, xyk]
                zf, zfk = ZZ[zi], ("zz", zi)
                if LIM <= 4:
                    continue
                psE, kE = self.big_ps()
                self.MM(psE[:, 0:128], IDB[:, :], IDB[:, :], ["idb"], [kE], start=True, stop=False)
                for e in range(2):
                    pe = slice(e * 64, (e + 1) * 64)
                    self.MM(psE[pe, e * 64:(e + 1) * 64], zf[:, e, 0:64], tm[:, 1, e * 64:(e + 1) * 64], [zfk, ("tm", par)], [kE],
                            start=False, stop=(e == 1))
                self.A(LBD[:, :], psE[:, 0:128], AF.Copy, [kE] + ppk, ["lbd"], scale=pprev)
                if LIM <= 5:
                    continue
                psQ, kQ = self.big_ps()
                for e in range(2):
                    pe = slice(e * 64, (e + 1) * 64)
                    self.MM(psQ[pe, 0:128], zf[:, e, 0:64], am[:, e, 128:256], [zfk, ("am", par, e)], [kQ], start=True, stop=False)
                self.MM(psQ[:, 0:128], IDB[:, :], sc[:, ch, C:2 * C], ["idb", sck], [kQ], start=False, stop=True)
                self.A(QP[:, :], psQ[:, 0:128], AF.Copy, [kQ] + ppk, ["qp"], scale=pprev)
                if LIM <= 6:
                    continue
                for e in range(2):
                    pe = slice(e * 64, (e + 1) * 64)
                    po = slice((1 - e) * 64, (2 - e) * 64)
                    psO, kO = self.big_ps()
                    oc_ = slice(0, 128)
                    self.MM(psO[:, oc_], stp[pe, :], QP[pe, :], [spk, "qp"], [kO], start=True, stop=False)
                    self.MM(psO[pe, oc_], zf[:, e, 64:128], am[:, e, 128:256], [zfk, ("am", par, e)], [kO], start=False, stop=False)
                    self.MM(psO[pe, oc_], tm[:, 3, e * 64:(e + 1) * 64], am[:, e, 384:512], [("tm", par), ("am", par, e)], [kO], start=False, stop=True)
                    self.CP("act", Y[b][pe, ch * C:(ch + 1) * C], psO[pe, 0:128], [kO], [("y", b)])
                    self.CP("dve", ZT[b][po, ch * C:(ch + 1) * C], psO[po, 0:128], [kO], [("zt", b)])
                if LIM <= 7:
                    continue
                psS, kS = self.big_ps()
                self.MM(psS[:, 0:128], LBD[:, :], stp[:, :], ["lbd", spk], [kS], start=True, stop=False)
                for e in range(2):
                    pe = slice(e * 64, (e + 1) * 64)
                    self.MM(psS[pe, e * 64:(e + 1) * 64], tm[:, 1, e * 64:(e + 1) * 64], zf[:, e, 64:128], [("tm", par), zfk], [kS], start=False, stop=False)
                    self.MM(psS[pe, e * 64:(e + 1) * 64], tm[:, 2, e * 64:(e + 1) * 64], tm[:, 3, e * 64:(e + 1) * 64], [("tm", par)], [kS],
                            start=False, stop=(e == 1))
                self.CP("act", stn[:, :], psS[:, 0:128], [kS], [snk])
                if ch == NCS - 2:
                    self.TS("dve", SEND[:, hp, :], stn[:, :], PC[b][:, ch:ch + 1], ALU.mult, [snk, ("pc", b)], [("send", hp)])
            P.dma("sp", self.d["y"][hp * 128:(hp + 1) * 128, :], Y[b][:, :], reads=[("y", b)], writes=[("d_y", hp)])
            P.dma("sp", self.d["z"][hp * 128:(hp + 1) * 128, :], ZT[b][:, :], reads=[("zt", b)], writes=[("d_z", hp)])
        P.dma("sp", self.d["send"], SEND[:, :, :].rearrange("p h n -> p (h n)"), reads=[("send", hp) for hp in range(c.NP)], writes=["d_send"])

    def rwkv_post(self, st):
        c, P = self.c, self.P
        T, SO, TSP, TS, C, KC, NP = c.T, c.SO, c.TSP, c.TS, c.C, c.KC, c.NP
        cb = c.cb
        CF = self.CF
        identf = CF[:, cb["ident"]:cb["ident"] + 128]
        blk = CF[:, cb["blk"]:cb["blk"] + 128]
        GA = [self.sb(st, [128, NP, 128], F32, "ga") for _ in range(3)]
        FT = self.sb(st, [128, NP, 128], F32, "ft")
        RT = [self.sb(st, [128, 128], F32, "rt") for _ in range(2)]
        FA = [self.sb(st, [128, 128], F32, "fa") for _ in range(2)]
        TD = [self.sb(st, [128, 128], F32, "td") for _ in range(2)]
        CB = self.sb(st, [128, NP, 128], BF16, "cbm")
        ablk = CF[:, cb["ablk"]:cb["ablk"] + 128]
        aif = CF[:, cb["ai"]:cb["ai"] + 128]
        for j in range(3):
            P.dma("sp", GA[j][:, :, :].rearrange("p h n -> p (h n)"), self.d["gath"][j], reads=["d_gath"], writes=[("ga", j)])
        self.MS("pool", FT[:, :, :], 0.0, ["ft"])
        for j in range(3):
            for hp in range(NP):
                i = hp % 2
                ps, pk = self.big_ps()
                self.MM(ps[:, 0:128], GA[j][:, hp, :], identf, [("ga", j), "cf"], [pk])
                self.TT("dve", RT[i][:, :], ps[:, 0:128], ablk, ALU.mult, [pk, "cf"], [("rt", i)])
                ps2, pk2 = self.big_ps()
                self.MM(ps2[:, 0:128], RT[i][:, :], FT[:, hp, :], [("rt", i), "ft"], [pk2])
                self.TT("pool", TD[i][:, :], GA[j][:, hp, :], blk, ALU.mult, [("ga", j), "cf"], [("td", i)])
                self.TT("dve", FA[i][:, :], ps2[:, 0:128], TD[i][:, :], ALU.add, [pk2, ("td", i)], [("fa", i)])
                ps3, pk3 = self.big_ps()
                self.MM(ps3[:, 0:128], aif, FA[i][:, :], [("fa", i), "cf"], [pk3])
                self.TT("dve", TD[i][:, :], ps3[:, 0:128], FT[:, hp, :], ALU.subtract, [pk3, "ft"], [("td", i)])
                self.STT(FT[:, hp, :], TD[i][:, :], self.CM[:, 1 + j:2 + j], FT[:, hp, :], ALU.mult, ALU.add, [("td", i), "ft", "cm"], ["ft"])
        self.CP("act", CB[:, :, :], FT[:, :, :], ["ft"], ["cbm"])

        YL = [self.sb(st, [128, TSP], F32, "yl") for _ in range(2)]
        ZL = [self.sb(st, [128, TSP], BF16, "zl") for _ in range(2)]
        BL = [self.sb(st, [128, TSP], F32, "bl") for _ in range(2)]
        GL = [self.sb(st, [128, T], F32, "gl") for _ in range(2)]
        YC = self.sb(st, [128, TSP], F32, "yc")
        SQ = self.sb(st, [128, TSP], F32, "sq2")
        RS = self.sb(st, [128, TSP], F32, "rs2")
        GO = [self.sb(st, [128, T], BF16, "go") for _ in range(2)]
        for b in range(2):
            self.MS("pool", GO[b][:, 0:SO], 0.0, [("go", b)])
        for hp in range(NP):
            b = hp % 2
            rows = slice(hp * 128, (hp + 1) * 128)
            P.dma("sp", YL[b][:, :], self.d["y"][rows, :], reads=[("d_y", hp)], writes=[("yl", b)])
            P.dma("sp", ZL[b][:, :], self.d["z"][rows, :], reads=[("d_z", hp)], writes=[("zl", b)])
            P.dma("sp", BL[b][:, :], self.d["bon"][rows, :], reads=[("d_bon", hp)], writes=[("bl", b)])
            P.dma("sp", GL[b][:, :], self.d["g"][rows, :], reads=[("d_g", hp)], writes=[("gl", b)])
            for o in range(0, TSP, 512):
                n_ = min(512, TSP - o)
                sl = slice(o, o + n_)
                ps, pk = self.big_ps()
                self.MM(ps[:, 0:n_], CB[:, hp, :], ZL[b][:, sl], ["cbm", ("zl", b)], [pk])
                self.TT("dve", YL[b][:, sl], ps[:, 0:n_], YL[b][:, sl], ALU.add, [pk, ("yl", b)], [("yl", b)])
                ps, pk = self.small_ps()
                self.MM(ps[:, 0:n_], blk, YL[b][:, sl], [("yl", b), "cf"], [pk])
                self.STT(YC[:, sl], ps[:, 0:n_], -1.0 / 64, YL[b][:, sl], ALU.mult, ALU.add, [pk, ("yl", b)], ["yc"])
                self.A(SQ[:, sl], YC[:, sl], AF.Square, ["yc"], ["sq2"])
                ps, pk = self.small_ps()
                self.MM(ps[:, 0:n_], blk, SQ[:, sl], ["sq2", "cf"], [pk])
                self.A(RS[:, sl], ps[:, 0:n_], AF.Sqrt, [pk, "epsg"], ["rs2"], scale=1.0 / 64, bias=self.EPSG[:, 0:1])
            P.op("dve", lambda e: e.reciprocal(out=RS[:, :], in_=RS[:, :]), ["rs2"], ["rs2"])
            self.TT("pool", YC[:, :], YC[:, :], RS[:, :], ALU.mult, ["yc", "rs2"], ["yc"])
            self.TS("pool", YC[:, :], YC[:, :], self.pv("gng", hp), ALU.mult, ["yc", "pv"], ["yc"], s2=self.pv("gnb", hp), op1=ALU.add)
            self.TT("pool", YC[:, :], YC[:, :], BL[b][:, :], ALU.add, ["yc", ("bl", b)], ["yc"])
            self.TT("dve", GO[b][:, SO:T], YC[:, 0:TS], GL[b][:, SO:T], ALU.mult, ["yc", ("gl", b)], [("go", b)])
            P.dma("sp", self.d["gated"][rows, :], GO[b][:, :], reads=[("go", b)], writes=[("d_gated", hp)])

    def build(self):
        c, nc, P = self.c, self.nc, self.P
        D, T, KC, FC = c.D, c.T, c.KC, c.FC
        mode = self.mode
        ext = lambda name, shape, dt=F32: self.dt_(name, shape, dt, "ExternalInput")
        pv_in = ext("pv", [128, c.NPV])
        cm_in = ext("cm", [128, 4])
        cf_in = ext("consts", [128, c.NCONST])
        w = {}
        inA = mode in ("A", "fused")
        inB = mode in ("B", "fused")
        if inA:
            xT = ext("xT", [D, T])
            w["sc_w_in"] = ext("sc_w_in", [D, 3 * D])
            w["sc_w_out"] = ext("sc_w_out", [D, D])
            w["ffn_up0"] = ext("ffn_up0", [D, 2 * c.F])
            w["ffn_dn0"] = ext("ffn_dn0", [c.F, D])
            for n, sh in (("rw_wr", [D, D]), ("rw_wk", [D, D]), ("rw_wv", [D, D]), ("rw_w1", [D, c.LW]), ("rw_w2", [c.LW, D]),
                          ("rw_a1", [D, c.LA]), ("rw_a2", [c.LA, D]), ("rw_g1", [D, c.LG]), ("rw_g2", [c.LG, D])):
                w[n] = ext(n, sh)
        if inB:
            w["rw_wo"] = ext("rw_wo", [D, D])
            w["ffn_up1"] = ext("ffn_up1", [D, 2 * c.F])
            w["ffn_dn1"] = ext("ffn_dn1", [c.F, D])
            outT = self.dt_("outT", [D, c.TOK], F32, "ExternalOutput")
        d = self.d = {}
        d["xres"] = self.handoff("xres", [D, T], F32, "A")
        for n in ("r", "k", "v", "sg", "a"):
            d[n] = self.dt_("d_" + n, [D, T], F32, "ExternalOutput" if self.stop_after == "proj" else "Internal") if inA else None
        d["g"] = self.handoff("d_g", [D, T], F32, "A")
        d["y"] = self.handoff("d_y", [D, c.TSP], F32, "A")
        d["z"] = self.handoff("d_z", [D, c.TSP], BF16, "A")
        d["bon"] = self.handoff("d_bon", [D, c.TSP], F32, "A")
        d["send"] = self.handoff("send", [128, c.NP * 128], F32, "A") if mode != "B" else None
        if mode == "B":
            d["gath"] = ext("gath", [3, 128, c.NP * 128])
        elif mode == "fused":
            d["gath"] = self.dt_("gath", [4, 128, c.NP * 128], F32, "Internal")
        if inB:
            d["gated"] = self.dt_("d_gated", [D, T], BF16, "Internal")
            d["xres2"] = self.dt_("xres2", [D, T], F32, "Internal")

        with contextlib.ExitStack() as g:
            self.ps = [g.enter_context(nc.psum_tensor("ps%d" % i, [128, 512], F32)) for i in range(8)]
            self.PV = self.sb(g, [128, c.NPV], F32, "pv")
            self.CM = self.sb(g, [128, 4], F32, "cm")
            self.CF = self.sb(g, [128, c.NCONST], F32, "cf")
            self.IDB = self.sb(g, [128, 128], BF16, "idb")
            self.ONES = self.sb(g, [128, 128], F32, "ones")
            self.ONE1 = self.sb(g, [128, 1], F32, "one1")
            self.EPSR = self.sb(g, [128, 1], F32, "epsr")
            self.EPSG = self.sb(g, [128, 1], F32, "epsg")
            P.dma("sp", self.PV[:, :], pv_in, writes=["pv"])
            P.dma("sp", self.CM[:, :], cm_in, writes=["cm"])
            P.dma("sp", self.CF[:, :], cf_in, writes=["cf"])
            self.CP("dve", self.IDB[:, :], self.CF[:, 0:128], ["cf"], ["idb"])
            self.MS("pool", self.ONES[:, :], 1.0, ["ones"])
            self.MS("pool", self.ONE1[:, :], 1.0, ["one1"])
            self.MS("pool", self.EPSR[:, :], RMS_EPS, ["epsr"])
            self.MS("pool", self.EPSG[:, :], GN_EPS, ["epsg"])

            if inA:
                with self.scope() as sH:
                    H = self.sb(sH, [128, KC, T], BF16, "H")
                    with self.scope() as s1:
                        M = self.sb(s1, [128, KC, T], F32, "M")
                        for kc in range(KC):
                            P.dma("sp", M[:, kc, :], xT[kc * 128:(kc + 1) * 128, :], writes=[("M", kc)])
                        self.norm(M, H, "ng00")
                        with self.scope() as s2:
                            self.sconv(s2, H, M, w["sc_w_in"], w["sc_w_out"])
                        self.residual(M, xT, d["xres"], "ng01", "xT", "xres")
                        self.norm(M, H, "ng02")
                        with self.scope() as s3:
                            self.ffn(s3, 0, H, M, w["ffn_up0"], w["ffn_dn0"])
                        self.residual(M, d["xres"], d["xres"], "ng03", "xres", "xres")
                        self.norm(M, H, "ng10")
                    if self.stop_after != "l0":
                        with self.scope() as s4:
                            self.set_w(s4, 4096)
                            self.rwkv_proj(s4, H, w)
                if self.stop_after not in ("l0", "proj"):
                    with self.scope() as s5:
                        self.rwkv_scan(s5)
            if mode == "fused":
                P.op("pool", lambda e: e.collective_compute("AllGather", ALU.bypass, [[0, 1, 2, 3], [4, 5, 6, 7]],
                                                            ins=[d["send"]], outs=[d["gath"].rearrange("j p n -> (j p) n")]),
                     reads=["d_send"], writes=["d_gath"], dma=True)
                P.barrier()
            if inB:
                with self.scope() as s6:
                    self.rwkv_post(s6)
                with self.scope() as s7:
                    H = self.sb(s7, [128, KC, T], BF16, "H")
                    M = self.sb(s7, [128, KC, T], F32, "M")
                    with self.scope() as s8:
                        self.set_w(s8, 4096)
                        G = self.sb(s8, [128, KC, T], BF16, "G")
                        for kc in range(KC):
                            P.dma("sp", G[:, kc, :], d["gated"][kc * 128:(kc + 1) * 128, :], reads=[("d_gated", kc)], writes=[("G", kc)])
                        nb = min(2, KC)
                        W = w["rw_wo"]
                        blocks = [(W[:, o * 128:(o + nb) * 128].rearrange("(k p) n -> p k n", p=128), [128] * nb) for o in range(0, KC, nb)]

                        def epi(bi, ci, pss):
                            oc = bi * nb + ci
                            for tt, (ps, pk) in enumerate(pss):
                                self.CP("act", M[:, oc, self.tts(tt)], ps[:, 0:c.TT], [pk], [("M", oc)])
                        self.linear(lambda kc, tt: (G[:, kc, self.tts(tt)], [("G", kc)]), KC, 128, blocks, epi)
                    self.residual(M, d["xres"], d["xres2"], "ng11", "xres", "xres2")
                    self.norm(M, H, "ng12")
                    with self.scope() as s9:
                        self.ffn(s9, 1, H, M, w["ffn_up1"], w["ffn_dn1"])
                    self.residual(M, d["xres2"], outT, "ng13", "xres2", "out", final=True)
            P.emit()
        return nc


def _core_inputs(c, inp, n_b, n_q):
    x = np.asarray(inp["x"], np.float32)
    pv = pack_pv(c, inp)
    consts = make_consts(c)
    cores = []
    for b in range(n_b):
        for q in range(n_q):
            s = q * c.TOK
            xs = np.zeros((c.T, c.D), np.float32)
            lo = max(0, s - c.HALO)
            xs[c.HALO - (s - lo):] = x[b, lo:s + c.TOK]
            cm = np.zeros((128, 4), np.float32)
            cm[:, 0] = 0.0 if q == 0 else 1.0
            for j in range(3):
                cm[:, 1 + j] = 1.0 if j < q else 0.0
            cores.append(dict(xT=np.ascontiguousarray(xs.T), pv=pv, cm=cm, consts=consts))
    return cores


WA = ("sc_w_in", "sc_w_out", "rw_wr", "rw_wk", "rw_wv", "rw_w1", "rw_w2", "rw_a1", "rw_a2", "rw_g1", "rw_g2")


def run_module(c, inp, n_b, n_q, fused=False):
    n = n_b * n_q
    cores = _core_inputs(c, inp, n_b, n_q)
    f32 = lambda a: np.ascontiguousarray(np.asarray(a, np.float32))
    wA = {k: f32(inp[k][0]) for k in WA}
    wA["ffn_up0"] = f32(inp["ffn_w_up"][0]); wA["ffn_dn0"] = f32(inp["ffn_w_down"][0])
    wB = {"rw_wo": f32(inp["rw_wo"][0]), "ffn_up1": f32(inp["ffn_w_up"][1]), "ffn_dn1": f32(inp["ffn_w_down"][1])}
    if fused:
        nc = Builder(c, "fused").build()
        maps = [dict(cores[i], **wA, **wB) for i in range(n)]
        res = run_bass_kernel_spmd(nc, maps, core_ids=list(range(n)))
        outs = [r["outT"] for r in res.results]
    else:
        ncA = Builder(c, "A").build()
        maps = [dict(cores[i], **wA) for i in range(n)]
        resA = run_bass_kernel_spmd(ncA, maps, core_ids=list(range(n))).results
        ncB = Builder(c, "B").build()
        mapsB = []
        for i in range(n):
            b = i // n_q
            gath = np.zeros((3, 128, c.NP * 128), np.float32)
            for j in range(min(3, n_q)):
                gath[j] = resA[b * n_q + j]["send"]
            m = dict(pv=cores[i]["pv"], cm=cores[i]["cm"], consts=cores[i]["consts"], gath=gath, **wB)
            for k in ("xres", "d_g", "d_y", "d_z", "d_bon"):
                m[k] = resA[i][k]
            mapsB.append(m)
        resB = run_bass_kernel_spmd(ncB, mapsB, core_ids=list(range(n))).results
        outs = [r["outT"] for r in resB]
    B = n_b
    out = np.zeros((B, n_q * c.TOK, c.D), np.float32)
    for i in range(n):
        b, q = divmod(i, n_q)
        out[b, q * c.TOK:(q + 1) * c.TOK] = outs[i].T
    return out


def kernel(**inputs):
    c = Cfg()
    return run_module(c, inputs, 2, 4, fused=False)
```
